# Optimizing a Trainium2 kernel written in Bass

```python
import math
import jax
import jax.numpy as jnp
from jax import lax
import numpy as np


D_MODEL = 2048
BATCH = 2
SEQ = 4096
DEPTH = 4

N_EVEN = (DEPTH + 1) // 2
N_ODD = DEPTH // 2
EPS = 1e-6

S5_WIDTH = D_MODEL // 2
S5_GROUP_SIZE = 16
S5_GROUPS = S5_WIDTH // S5_GROUP_SIZE
S5_STATE = 64
S5_MIN_DECAY = 1e-4

HGRN_WIDTH = D_MODEL - S5_WIDTH
HGRN_HEAD_DIM = 128
HGRN_HEADS = HGRN_WIDTH // HGRN_HEAD_DIM
HGRN_CHUNK = 64

IN_EVEN = S5_WIDTH + 4 * HGRN_WIDTH

ATT_HEAD_DIM = 64
ATT_HEADS = D_MODEL // ATT_HEAD_DIM
ATT_GROUP = 8
ATT_KV_HEADS = ATT_HEADS // ATT_GROUP
WINDOW = 128
ATT_BLOCK = 128
QKV_WIDTH = (ATT_HEADS + 2 * ATT_KV_HEADS) * ATT_HEAD_DIM

D_FF = 4 * D_MODEL

kernel_name = 'hybrid_s5_hgrn2_swa_block'

F32 = jnp.float32


def rms_norm(x, gain):
    xf = x.astype(F32)
    y = xf * lax.rsqrt(jnp.mean(xf * xf, axis=-1, keepdims=True) + EPS)
    return (y * gain.astype(F32)).astype(x.dtype)


def alibi_slopes(n_heads):
    return jnp.exp2(-8.0 * jnp.arange(1, n_heads + 1, dtype=F32) / n_heads)


def hgrn_lower_bounds(lb_param):
    p = jax.nn.softmax(lb_param.astype(F32), axis=0)
    return jnp.cumsum(p, axis=0) - p[0:1]


def s5_mixer(u, lam_re, lam_im, log_dt, b_re, b_im, c_re, c_im, d_skip, w_glu, b_glu):
    bsz, seqlen, _ = u.shape
    uf = u.astype(F32).reshape(bsz, seqlen, S5_GROUPS, S5_GROUP_SIZE)
    lr = jnp.minimum(lam_re.astype(F32), -S5_MIN_DECAY)
    li = lam_im.astype(F32)
    dt = jnp.exp(log_dt.astype(F32))[:, None]
    mag = jnp.exp(lr * dt)
    ar = mag * jnp.cos(li * dt)
    ai = mag * jnp.sin(li * dt)
    den = lr * lr + li * li
    zr = ((ar - 1.0) * lr + ai * li) / den
    zi = (ai * lr - (ar - 1.0) * li) / den
    br, bi = b_re.astype(F32), b_im.astype(F32)
    bbr = zr[..., None] * br - zi[..., None] * bi
    bbi = zr[..., None] * bi + zi[..., None] * br
    bu_re = jnp.einsum('blgc,gpc->blgp', uf, bbr)
    bu_im = jnp.einsum('blgc,gpc->blgp', uf, bbi)
    a_re = jnp.broadcast_to(ar, bu_re.shape)
    a_im = jnp.broadcast_to(ai, bu_im.shape)

    def combine(e1, e2):
        a1r, a1i, b1r, b1i = e1
        a2r, a2i, b2r, b2i = e2
        return (a2r * a1r - a2i * a1i,
                a2r * a1i + a2i * a1r,
                a2r * b1r - a2i * b1i + b2r,
                a2r * b1i + a2i * b1r + b2i)

    _, _, x_re, x_im = lax.associative_scan(combine, (a_re, a_im, bu_re, bu_im), axis=1)
    y = (jnp.einsum('blgp,gcp->blgc', x_re, c_re.astype(F32))
         - jnp.einsum('blgp,gcp->blgc', x_im, c_im.astype(F32))
         + d_skip.astype(F32) * uf)
    y = jax.nn.gelu(y.reshape(bsz, seqlen, S5_WIDTH))
    return y * jax.nn.sigmoid(y @ w_glu.astype(F32) + b_glu.astype(F32))


def chunkwise_gated_recurrence(q, log_f, k, v):
    bsz, seqlen, heads, dk = q.shape
    dv = v.shape[-1]
    n_chunks = seqlen // HGRN_CHUNK

    def to_chunks(t):
        return t.reshape(bsz, n_chunks, HGRN_CHUNK, heads, t.shape[-1]).transpose(1, 0, 3, 2, 4)

    qc, gc, kc, vc = to_chunks(q), to_chunks(log_f), to_chunks(k), to_chunks(v)
    causal = jnp.tril(jnp.ones((HGRN_CHUNK, HGRN_CHUNK), dtype=bool))[:, :, None]

    def step(state, inputs):
        q_, g_, k_, v_ = inputs
        b = jnp.cumsum(g_, axis=2)
        diff = b[:, :, :, None, :] - b[:, :, None, :, :]
        decay = jnp.exp(jnp.where(causal, diff, -jnp.inf))
        scores = jnp.einsum('bhtk,bhsk,bhtsk->bhts', q_, k_, decay)
        out = (jnp.einsum('bhts,bhsv->bhtv', scores, v_)
               + jnp.einsum('bhtk,bhkv->bhtv', q_ * jnp.exp(b), state))
        b_last = b[:, :, -1:, :]
        state = (jnp.exp(b_last[:, :, 0, :])[..., None] * state
                 + jnp.einsum('bhsk,bhsv->bhkv', k_ * jnp.exp(b_last - b), v_))
        return state, out

    state0 = jnp.zeros((bsz, heads, dk, dv), F32)
    _, outs = lax.scan(step, state0, (qc, gc, kc, vc))
    return outs.transpose(1, 0, 3, 2, 4).reshape(bsz, seqlen, heads, dv)


def hgrn2_mixer(q, f, i, g, lower_bound, o_gain):
    bsz, seqlen, _ = q.shape

    def heads(t):
        return t.astype(F32).reshape(bsz, seqlen, HGRN_HEADS, HGRN_HEAD_DIM)

    lb = lower_bound.reshape(HGRN_HEADS, HGRN_HEAD_DIM)
    forget = lb + (1.0 - lb) * jax.nn.sigmoid(heads(f))
    o = chunkwise_gated_recurrence(jax.nn.silu(heads(q)), jnp.log(forget), 1.0 - forget, heads(i))
    o = rms_norm(o, o_gain) * jax.nn.silu(heads(g))
    return o.reshape(bsz, seqlen, HGRN_WIDTH)


def s5_hgrn2_layer(xn, w_in, lam_re, lam_im, log_dt, b_re, b_im, c_re, c_im, d_skip,
                   w_glu, b_glu, lower_bound, o_gain, w_out):
    proj = xn @ w_in
    u, q, f, i, g = jnp.split(proj, [S5_WIDTH, S5_WIDTH + HGRN_WIDTH,
                                     S5_WIDTH + 2 * HGRN_WIDTH, S5_WIDTH + 3 * HGRN_WIDTH], axis=-1)
    y_a = s5_mixer(u, lam_re, lam_im, log_dt, b_re, b_im, c_re, c_im, d_skip, w_glu, b_glu)
    y_b = hgrn2_mixer(q, f, i, g, lower_bound, o_gain)
    return jnp.concatenate([y_a, y_b], axis=-1).astype(xn.dtype) @ w_out


def sliding_window_attention(q, k, v, sinks):
    bsz, seqlen = q.shape[:2]
    nb = seqlen // ATT_BLOCK
    qb = q.reshape(bsz, nb, ATT_BLOCK, ATT_KV_HEADS, ATT_GROUP, ATT_HEAD_DIM)

    def band(t):
        tb = t.reshape(bsz, nb, ATT_BLOCK, ATT_KV_HEADS, ATT_HEAD_DIM)
        prev = jnp.pad(tb, ((0, 0), (1, 0), (0, 0), (0, 0), (0, 0)))[:, :-1]
        return jnp.concatenate([prev, tb], axis=2)

    kw, vw = band(k), band(v)
    scores = jnp.einsum('bnqkgd,bnskd->bnkgqs', qb, kw) * (ATT_HEAD_DIM ** -0.5)
    dist = jnp.arange(ATT_BLOCK)[:, None] + ATT_BLOCK - jnp.arange(2 * ATT_BLOCK)[None, :]
    in_window = (dist >= 0) & (dist < WINDOW)
    key_pos = (jnp.arange(nb)[:, None] * ATT_BLOCK - ATT_BLOCK
               + jnp.arange(2 * ATT_BLOCK)[None, :])
    mask = in_window[None] & (key_pos >= 0)[:, None, :]
    slopes = alibi_slopes(ATT_HEADS).reshape(ATT_KV_HEADS, ATT_GROUP)
    bias = -slopes[:, :, None, None] * dist.astype(F32)
    scores = jnp.where(mask[None, :, None, None], scores + bias, -jnp.inf)
    sink = jnp.broadcast_to(sinks.astype(F32).reshape(ATT_KV_HEADS, ATT_GROUP)[None, None, :, :, None, None],
                            scores.shape[:-1] + (1,))
    probs = jax.nn.softmax(jnp.concatenate([scores, sink], axis=-1), axis=-1)[..., :-1]
    out = jnp.einsum('bnkgqs,bnskd->bnqkgd', probs, vw)
    return out.reshape(bsz, seqlen, ATT_HEADS * ATT_HEAD_DIM)


def attention_layer(xn, w_qkv, q_gain, k_gain, sinks, w_out):
    bsz, seqlen, _ = xn.shape
    proj = xn @ w_qkv
    q, k, v = jnp.split(proj, [ATT_HEADS * ATT_HEAD_DIM, (ATT_HEADS + ATT_KV_HEADS) * ATT_HEAD_DIM], axis=-1)
    q = rms_norm(q.astype(F32).reshape(bsz, seqlen, ATT_HEADS, ATT_HEAD_DIM), q_gain)
    k = rms_norm(k.astype(F32).reshape(bsz, seqlen, ATT_KV_HEADS, ATT_HEAD_DIM), k_gain)
    v = v.astype(F32).reshape(bsz, seqlen, ATT_KV_HEADS, ATT_HEAD_DIM)
    o = sliding_window_attention(q, k, v, sinks)
    return o.astype(xn.dtype) @ w_out


def squared_relu_mlp(xn, w_up, w_down):
    return jnp.square(jax.nn.relu(xn @ w_up)) @ w_down


def setup_inputs(seed: int = 0) -> dict:
    key = jax.random.key(seed)
    ks = iter(jax.random.split(key, 32))

    def nrm(shape, scale):
        return scale * jax.random.normal(next(ks), shape, F32)

    x = nrm((BATCH, SEQ, D_MODEL), 1.0)
    even_norm = 1.0 + nrm((N_EVEN, D_MODEL), 0.02)
    even_w_in = nrm((N_EVEN, D_MODEL, IN_EVEN), D_MODEL ** -0.5)
    s5_lambda_re = -0.5 + nrm((N_EVEN, S5_GROUPS, S5_STATE), 0.01)
    s5_lambda_im = math.pi * jnp.arange(S5_STATE, dtype=F32) + nrm((N_EVEN, S5_GROUPS, S5_STATE), 0.01)
    s5_log_dt = jax.random.uniform(next(ks), (N_EVEN, S5_GROUPS), F32, math.log(1e-3), math.log(1e-1))
    s5_b_re = nrm((N_EVEN, S5_GROUPS, S5_STATE, S5_GROUP_SIZE), (2 * S5_GROUP_SIZE) ** -0.5)
    s5_b_im = nrm((N_EVEN, S5_GROUPS, S5_STATE, S5_GROUP_SIZE), (2 * S5_GROUP_SIZE) ** -0.5)
    s5_c_re = nrm((N_EVEN, S5_GROUPS, S5_GROUP_SIZE, S5_STATE), S5_STATE ** -0.5)
    s5_c_im = nrm((N_EVEN, S5_GROUPS, S5_GROUP_SIZE, S5_STATE), S5_STATE ** -0.5)
    s5_d = nrm((N_EVEN, S5_GROUPS, S5_GROUP_SIZE), 1.0)
    s5_w_glu = nrm((N_EVEN, S5_WIDTH, S5_WIDTH), S5_WIDTH ** -0.5)
    s5_b_glu = nrm((N_EVEN, S5_WIDTH), 0.01)
    hgrn_lower_bound = nrm((N_EVEN, HGRN_WIDTH), 0.1)
    hgrn_o_norm = 1.0 + nrm((N_EVEN, HGRN_HEAD_DIM), 0.02)
    even_w_out = nrm((N_EVEN, D_MODEL, D_MODEL), D_MODEL ** -0.5)
    odd_norm = 1.0 + nrm((N_ODD, D_MODEL), 0.02)
    odd_w_qkv = nrm((N_ODD, D_MODEL, QKV_WIDTH), D_MODEL ** -0.5)
    q_norm = 1.0 + nrm((N_ODD, ATT_HEAD_DIM), 0.02)
    k_norm = 1.0 + nrm((N_ODD, ATT_HEAD_DIM), 0.02)
    att_sinks = nrm((N_ODD, ATT_HEADS), 0.5)
    odd_w_out = nrm((N_ODD, D_MODEL, D_MODEL), D_MODEL ** -0.5)
    mlp_norm = 1.0 + nrm((DEPTH, D_MODEL), 0.02)
    mlp_w_up = nrm((DEPTH, D_MODEL, D_FF), D_MODEL ** -0.5)
    mlp_w_down = nrm((DEPTH, D_FF, D_MODEL), D_FF ** -0.5)
    return {'x': x, 'even_norm': even_norm, 'even_w_in': even_w_in,
            's5_lambda_re': s5_lambda_re, 's5_lambda_im': s5_lambda_im, 's5_log_dt': s5_log_dt,
            's5_b_re': s5_b_re, 's5_b_im': s5_b_im, 's5_c_re': s5_c_re, 's5_c_im': s5_c_im,
            's5_d': s5_d, 's5_w_glu': s5_w_glu, 's5_b_glu': s5_b_glu,
            'hgrn_lower_bound': hgrn_lower_bound, 'hgrn_o_norm': hgrn_o_norm, 'even_w_out': even_w_out,
            'odd_norm': odd_norm, 'odd_w_qkv': odd_w_qkv, 'q_norm': q_norm, 'k_norm': k_norm,
            'att_sinks': att_sinks, 'odd_w_out': odd_w_out,
            'mlp_norm': mlp_norm, 'mlp_w_up': mlp_w_up, 'mlp_w_down': mlp_w_down}


def reference(x, even_norm, even_w_in, s5_lambda_re, s5_lambda_im, s5_log_dt, s5_b_re, s5_b_im,
              s5_c_re, s5_c_im, s5_d, s5_w_glu, s5_b_glu, hgrn_lower_bound, hgrn_o_norm, even_w_out,
              odd_norm, odd_w_qkv, q_norm, k_norm, att_sinks, odd_w_out, mlp_norm, mlp_w_up, mlp_w_down):
    h = x
    lower_bounds = hgrn_lower_bounds(hgrn_lower_bound)
    for layer in range(DEPTH):
        j = layer // 2
        if layer % 2 == 0:
            y = s5_hgrn2_layer(rms_norm(h, even_norm[j]), even_w_in[j], s5_lambda_re[j], s5_lambda_im[j],
                               s5_log_dt[j], s5_b_re[j], s5_b_im[j], s5_c_re[j], s5_c_im[j], s5_d[j],
                               s5_w_glu[j], s5_b_glu[j], lower_bounds[j], hgrn_o_norm[j], even_w_out[j])
        else:
            y = attention_layer(rms_norm(h, odd_norm[j]), odd_w_qkv[j], q_norm[j], k_norm[j],
                                att_sinks[j], odd_w_out[j])
        h = h + y.astype(h.dtype)
        h = h + squared_relu_mlp(rms_norm(h, mlp_norm[layer]), mlp_w_up[layer], mlp_w_down[layer]).astype(h.dtype)
    return h
```

```python
from contextlib import ExitStack
import numpy as np
import concourse.bass as bass
import concourse.mybir as mybir
from concourse.bass_utils import run_bass_kernel_spmd

F32 = mybir.dt.float32
BF16 = mybir.dt.bfloat16
I32 = mybir.dt.int32
ALU = mybir.AluOpType
AF = mybir.ActivationFunctionType
AX = mybir.AxisListType

ENGS = ("pe", "act", "dve", "pool", "sp")


class Buf:
    __slots__ = ("name", "w", "r")

    def __init__(self, name):
        self.name = name
        self.w = None
        self.r = {}


class Prog:
    def __init__(self, nc, es):
        self.nc = nc
        self.es = es
        self.q = {e: [] for e in ENGS}
        self.sem = {e: es.enter_context(nc.semaphore("s_" + e)) for e in ENGS}
        self.chan = {}
        self.bufs = {}
        self.bends = {}

    def buf(self, name):
        b = self.bufs.get(name)
        if b is None:
            b = self.bufs[name] = Buf(name)
        return b

    def _chan(self, name):
        c = self.chan.get(name)
        if c is None:
            c = self.chan[name] = [self.es.enter_context(self.nc.semaphore("d_" + name)), 0]
        return c

    def end_batch(self, chan):
        self.bends.setdefault(chan, []).append(self.chan[chan][1])

    def _track(self, tok, eng, reads, writes):
        waits = set()
        for b in reads:
            if b.w is not None:
                waits.add(b.w)
        for b in writes:
            if b.w is not None:
                waits.add(b.w)
            waits.update(b.r.values())
        waits.discard(tok)
        if eng == "pe":
            waits = {t for t in waits if not (t[0] == "e" and t[1] == "pe")}
        key = tok[1]
        for b in reads:
            b.r[key] = tok
        for b in writes:
            b.w = tok
            b.r = {}
        return waits

    def op(self, eng, fn, reads=(), writes=(), sig=True):
        reads = [self.buf(b) if isinstance(b, str) else b for b in reads]
        writes = [self.buf(b) if isinstance(b, str) else b for b in writes]
        tok = ("e", eng, len(self.q[eng]))
        waits = self._track(tok, eng, reads, writes)
        self.q[eng].append((fn, waits, sig, None))

    def dma(self, eng, chan, out, in_, reads=(), writes=(), **kw):
        reads = [self.buf(b) if isinstance(b, str) else b for b in reads]
        writes = [self.buf(b) if isinstance(b, str) else b for b in writes]
        c = self._chan(chan)
        c[1] += 16
        tok = ("d", chan, c[1])
        waits = self._track(tok, eng, reads, writes)
        self.q[eng].append((lambda e: e.dma_start(out=out, in_=in_, **kw), waits, False, c[0]))
        return tok

    def fence(self):
        toks = []
        for e in ENGS:
            q = self.q[e]
            for i in range(len(q) - 1, -1, -1):
                if q[i][0] is not None and q[i][3] is None:
                    q[i] = (q[i][0], q[i][1], True, None)
                    toks.append(("e", e, i))
                    break
        for e in ENGS:
            waits = {t for t in toks if t[1] != e}
            self.q[e].append((None, waits, False, None))

    def emit(self, final_waits=()):
        nc = self.nc
        sigval = {}
        for e in ENGS:
            q = self.q[e]
            vals = [0] * len(q)
            cnt = 0
            cum = []
            for (fn, w, sig, dsem) in q:
                if sig:
                    cnt += 1
                cum.append(cnt)
            nxt = None
            for i in range(len(q) - 1, -1, -1):
                if q[i][2]:
                    nxt = cum[i]
                vals[i] = nxt
            sigval[e] = vals

        def resolve(t):
            if t[0] == "e":
                v = sigval[t[1]][t[2]]
                assert v is not None, ("dependency on non-signalling tail", t)
                return self.sem[t[1]], v, ("e", t[1])
            v = t[2]
            for be in self.bends.get(t[1], ()):
                if be >= v:
                    v = be
                    break
            return self.chan[t[1]][0], v, ("d", t[1])

        semv = {}
        pos = {e: 0 for e in ENGS}
        progress = True
        while progress:
            progress = False
            for en in ENGS:
                q = self.q[en]
                while pos[en] < len(q):
                    fn, waits, sig, dsem = q[pos[en]]
                    ok = True
                    for t in waits:
                        sm, v, k = resolve(t)
                        if semv.get(k, 0) < v:
                            ok = False
                            break
                    if not ok:
                        break
                    if dsem is not None:
                        ck = [k2 for k2, c in self.chan.items() if c[0] is dsem][0]
                        semv[("d", ck)] = semv.get(("d", ck), 0) + 16
                    elif sig and fn is not None:
                        semv[("e", en)] = semv.get(("e", en), 0) + 1
                    pos[en] += 1
                    progress = True
        for en in ENGS:
            if pos[en] < len(self.q[en]):
                fn, waits, sig, dsem = self.q[en][pos[en]]
                print("DEADLOCK", en, pos[en], len(self.q[en]), [(t, resolve(t)[1], semv.get(resolve(t)[2], 0)) for t in waits])
        assert all(pos[en] == len(self.q[en]) for en in ENGS), "static deadlock"

        with nc.Block() as block:
            def run(engname, e):
                waited = {}
                for (fn, waits, sig, dsem) in self.q[engname]:
                    need = {}
                    for t in waits:
                        s, v, k = resolve(t)
                        if waited.get(k, 0) >= v:
                            continue
                        if k not in need or need[k][1] < v:
                            need[k] = (s, v)
                    for k, (s, v) in need.items():
                        e.wait_ge(s, v)
                        waited[k] = v
                    if fn is None:
                        continue
                    inst = fn(e)
                    if dsem is not None:
                        inst.then_inc(dsem, 16)
                    elif sig:
                        inst.then_inc(self.sem[engname], 1)
                if engname == "sp":
                    for t in final_waits:
                        s, v, k = resolve(t)
                        e.wait_ge(s, v)

            @block.tensor
            def _(e):
                run("pe", e)

            @block.scalar
            def _(e):
                run("act", e)

            @block.vector
            def _(e):
                run("dve", e)

            @block.gpsimd
            def _(e):
                run("pool", e)

            @block.sync
            def _(e):
                run("sp", e)


D = 2048
KT = 16
TOK = 1024
HF = 512
DFF = 8192
EPS = 1e-6
WC = 256
PI = float(np.pi)


class Ctx:
    pass


def setup(nc, es, nseg, T):
    C = Ctx()
    C.nc = nc
    C.es = es
    C.P = Prog(nc, es)
    C.nseg = nseg
    C.T = T
    C.uid = 0

    def sb(name, shape, dt, stack=None):
        C.uid += 1
        return (stack or es).enter_context(nc.sbuf_tensor("sb%d_%s" % (C.uid, name), shape, dt))

    C.sb = sb
    C.hT = sb("hT", [128, KT, TOK], F32)
    C.xn = sb("xn", [128, KT, TOK], BF16)
    C.yT = sb("yT", [128, KT, TOK], BF16)
    C.NW = 2
    C.w = [sb("w%d" % i, [128, KT, WC], BF16) for i in range(C.NW)]
    C.wi = 0
    C.sq = [sb("sq%d" % i, [128, HF], BF16) for i in range(2)]
    C.sqi = 0
    C.rstd = sb("rstd", [128, HF], F32)
    C.tmpf = sb("tmpf", [128, HF], F32)
    C.tmpb = [sb("tmpb%d" % i, [128, HF], BF16) for i in range(2)]
    C.ones = sb("ones", [128, 128], BF16)
    C.ident = sb("ident", [128, 128], BF16)
    C.bones = sb("bones", [128, 128], BF16)
    C.gains = sb("gains", [128, 8, KT], F32)
    C.epsc = sb("epsc", [128, 1], F32)
    C.ps = [es.enter_context(nc.psum_tensor("ps%d" % i, [128, 512], F32)) for i in range(5)]
    C.psx = [es.enter_context(nc.psum_tensor("psx%d" % i, [128, 512], F32)) for i in range(2)]
    C.psT = es.enter_context(nc.psum_tensor("psT", [128, 1024], BF16))
    C.psi = 0
    return C


def next_ps(C):
    i = C.psi
    C.psi = (C.psi + 1) % len(C.ps)
    return C.ps[i], "ps%d" % i


def next_w(C):
    i = C.wi
    C.wi = (C.wi + 1) % C.NW
    return C.w[i], "w%d" % i


def load_w(C, src, kt, ncols, col0=0, w=None, wn=None):
    if w is None:
        w, wn = next_w(C)
    C.P.dma("pool", wn, w[:, 0:kt, col0:col0 + ncols], src.rearrange("(kt p) n -> p kt n", p=128), writes=[wn])
    return w, wn


def rmsnorm(C, gi):
    P = C.P
    for hf in range(2):
        hs = slice(hf * HF, (hf + 1) * HF)
        ps, pn = next_ps(C)
        for kt in range(KT):
            sq = C.sq[kt % 2]
            sn = "sq%d" % (kt % 2)
            P.op("act", lambda e, sq=sq, kt=kt, hs=hs: e.activation(out=sq[:], in_=C.hT[:, kt, hs], func=AF.Square),
                 reads=["hT%d.%d" % (kt, hf)], writes=[sn])
            P.op("pe", lambda e, sq=sq, kt=kt, ps=ps: e.matmul(ps[:], C.ones[:], sq[:], start=(kt == 0), stop=(kt == KT - 1)),
                 reads=[sn, "ones"], writes=[pn])
        P.op("act", lambda e, ps=ps: e.activation(out=C.tmpf[:], in_=ps[:], func=AF.Sqrt, bias=C.epsc[:, 0:1], scale=1.0 / D),
             reads=[pn, "epsc"], writes=["tmpf"])
        P.op("dve", lambda e: e.reciprocal(out=C.rstd[:], in_=C.tmpf[:]), reads=["tmpf"], writes=["rstd"])
        for kt in range(KT):
            P.op("dve", lambda e, kt=kt, hs=hs: e.scalar_tensor_tensor(out=C.xn[:, kt, hs], in0=C.hT[:, kt, hs], scalar=C.gains[:, gi, kt:kt + 1],
                                                                 in1=C.rstd[:], op0=ALU.mult, op1=ALU.mult),
                 reads=["hT%d.%d" % (kt, hf), "rstd", "gains"], writes=["xn%d.%d" % (kt, hf)])


def mm_fm(C, w, wn, c0, src, srcname, hf, nkt=KT):
    hs = slice(hf * HF, (hf + 1) * HF)
    ps, pn = next_ps(C)
    for kt in range(nkt):
        C.P.op("pe", lambda e, kt=kt: e.matmul(ps[:], w[:, kt, c0:c0 + 128], src[:, kt, hs], start=(kt == 0), stop=(kt == nkt - 1)),
               reads=[wn, "%s%d.%d" % (srcname, kt, hf)], writes=[pn], sig=(kt == nkt - 1))
    return ps, pn


def mlp(C, w_up, w_down):
    P = C.P
    hid = C.yT
    it = 0
    for c in range(4):
        for fb in range(8):
            f0 = c * 2048 + fb * WC
            w, wn = load_w(C, w_up[:, f0:f0 + WC], KT, WC)
            for m in range(2):
                ft = fb * 2 + m
                for hf in range(2):
                    hs = slice(hf * HF, (hf + 1) * HF)
                    ps, pn = mm_fm(C, w, wn, m * 128, C.xn, "xn", hf)
                    tb, tn = C.tmpb[it % 2], "tmpb%d" % (it % 2)
                    it += 1
                    P.op("act", lambda e, ps=ps, tb=tb: e.activation(out=tb[:], in_=ps[:], func=AF.Relu), reads=[pn], writes=[tn])
                    P.op("pool", lambda e, tb=tb, ft=ft, hs=hs: e.tensor_tensor(out=hid[:, ft, hs], in0=tb[:], in1=tb[:], op=ALU.mult),
                         reads=[tn], writes=["yT%d.%d" % (ft, hf)])
        proj_add(C, w_down[c * 2048:(c + 1) * 2048, :], hid, "yT")


def proj_add(C, wsrc, src, srcname):
    P = C.P
    for db in range(8):
        w, wn = load_w(C, wsrc[:, db * WC:(db + 1) * WC], KT, WC)
        for m in range(2):
            dt_ = db * 2 + m
            for hf in range(2):
                hs = slice(hf * HF, (hf + 1) * HF)
                ps, pn = mm_fm(C, w, wn, m * 128, src, srcname, hf)
                P.op("dve", lambda e, ps=ps, dt_=dt_, hs=hs: e.tensor_tensor(out=C.hT[:, dt_, hs], in0=C.hT[:, dt_, hs], in1=ps[:], op=ALU.add),
                     reads=[pn, "hT%d.%d" % (dt_, hf)], writes=["hT%d.%d" % (dt_, hf)])


def load_consts(C, dr):
    P = C.P
    P.op("dve", lambda e: e.memset(C.epsc[:], EPS), writes=["epsc"])
    P.dma("pool", "c_ones", C.ones[:], dr["c_ones"], writes=["ones"])
    P.dma("pool", "c_ident", C.ident[:], dr["c_ident"], writes=["ident"])
    P.dma("pool", "c_bones", C.bones[:], dr["c_bones"], writes=["bones"])
    P.dma("sp", "c_gains", C.gains[:], dr["gains"], writes=["gains"])


def load_x(C, xT, seg):
    for kt in range(KT):
        C.P.dma("sp", "xin", C.hT[:, kt, :], xT[kt * 128:(kt + 1) * 128, seg * TOK:(seg + 1) * TOK],
                writes=["hT%d.0" % kt, "hT%d.1" % kt])
    C.P.end_batch("xin")


def store_y(C, yT, seg):
    toks = []
    for kt in range(KT):
        toks.append(C.P.dma("sp", "yout", yT[kt * 128:(kt + 1) * 128, seg * TOK:(seg + 1) * TOK], C.hT[:, kt, :],
                            reads=["hT%d.0" % kt, "hT%d.1" % kt]))
    C.P.end_batch("yout")
    return toks


def qk_norm(C, ps, pn, gain_ap, out_ap, out_names):
    P = C.P
    sq, sn = C.sq[C.sqi % 2], "sq%d" % (C.sqi % 2)
    C.sqi += 1
    P.op("act", lambda e: e.activation(out=sq[:], in_=ps[:], func=AF.Square), reads=[pn], writes=[sn])
    p2, p2n = next_ps(C)
    P.op("pe", lambda e: e.matmul(p2[:], C.bones[:], sq[:], start=True, stop=True), reads=[sn, "bones"], writes=[p2n])
    P.op("act", lambda e: e.activation(out=C.tmpf[:], in_=p2[:], func=AF.Sqrt, bias=C.epsc[:, 0:1], scale=1.0 / 64),
         reads=[p2n, "epsc"], writes=["tmpf"])
    P.op("dve", lambda e: e.reciprocal(out=C.rstd[:], in_=C.tmpf[:]), reads=["tmpf"], writes=["rstd"])
    P.op("dve", lambda e: e.scalar_tensor_tensor(out=out_ap, in0=ps[:], scalar=gain_ap, in1=C.rstd[:], op0=ALU.mult, op1=ALU.mult),
         reads=[pn, "rstd", "attc"], writes=out_names)


def setup_swa(C, dr):
    sb = C.sb
    C.attc = sb("attc", [128, 2, 18], F32)
    C.khalo = [sb("khalo%d" % i, [128, 4, 128], BF16) for i in range(2)]
    C.vhalo = [sb("vhalo%d" % i, [128, 256], BF16) for i in range(2)]
    C.P.dma("sp", "c_attc", C.attc[:], dr["attc"], writes=["attc"])
    C.swa_it = 0


def swa_layer(C, j, seg, wqkv, wout, c_wt):
    P = C.P
    it = C.swa_it
    with ExitStack() as st:
        sb = lambda n, s, d: C.sb(n, s, d, st)
        esink = sb("esink", [128, 16], F32)
        khat = sb("khat", [128, 4, TOK + 128], BF16)
        vtok = sb("vtok", [128, 9, 256], BF16)
        qhat = [sb("qhat%d" % i, [128, TOK], BF16) for i in range(2)]
        pexp = [sb("pexp%d" % i, [128, 512], BF16) for i in range(2)]
        wts = [sb("wts%d" % i, [128, 512], BF16) for i in range(2)]
        rden = sb("rden", [128, 128], F32)
        L = "L%d_%d_" % (seg, j)
        P.op("act", lambda e: e.activation(out=esink[:], in_=C.attc[:, j, 2:18], func=AF.Exp), reads=["attc"], writes=[L + "esink"])
        if seg > 0:
            P.op("pool", lambda e: e.tensor_copy(out=khat[:, :, 0:128], in_=C.khalo[j][:]), reads=["khalo%d" % j], writes=[L + "khat"])
            P.op("pool", lambda e: e.tensor_copy(out=vtok[:, 0, :], in_=C.vhalo[j][:]), reads=["vhalo%d" % j], writes=[L + "vtok"])
        for kb in range(2):
            w, wn = load_w(C, wqkv[:, 2048 + kb * WC:2048 + (kb + 1) * WC], KT, WC)
            for m in range(2):
                kv = kb * 2 + m
                for hf in range(2):
                    ps, pn = mm_fm(C, w, wn, m * 128, C.xn, "xn", hf)
                    qk_norm(C, ps, pn, C.attc[:, j, 1:2], khat[:, kv, 128 + hf * HF:128 + (hf + 1) * HF], [L + "khat"])
        w, wn = load_w(C, wqkv[:, 2560:2816], KT, 256)
        for b in range(8):
            ps, pn = next_ps(C)
            for kt in range(KT):
                P.op("pe", lambda e, w=w, b=b, kt=kt, ps=ps: e.matmul(ps[:, 0:256], C.xn[:, kt, b * 128:(b + 1) * 128], w[:, kt, 0:256],
                                                                    start=(kt == 0), stop=(kt == KT - 1)),
                     reads=[wn, "xn%d.%d" % (kt, b // 4)], writes=[pn], sig=(kt == KT - 1))
            P.op("act", lambda e, b=b, ps=ps: e.activation(out=vtok[:, b + 1, :], in_=ps[:, 0:256], func=AF.Copy), reads=[pn], writes=[L + "vtok"])
        import os
        STG = int(os.environ.get("SWA_STAGE", "9"))
        for qb in range(8 if STG >= 2 else 0):
            w, wn = load_w(C, wqkv[:, qb * WC:(qb + 1) * WC], KT, WC)
            for m in range(2):
                i = qb * 2 + m
                kv = i // 4
                qh, qn = qhat[i % 2], L + "qhat%d" % (i % 2)
                wt, wtn = wts[i % 2], L + "wts%d" % (i % 2)
                P.dma("pool", "wts%d" % (i % 2), wt[:], c_wt[:, i * 512:(i + 1) * 512], writes=[wtn])
                for hf in range(2):
                    hs = slice(hf * HF, (hf + 1) * HF)
                    ps, pn = mm_fm(C, w, wn, m * 128, C.xn, "xn", hf)
                    qk_norm(C, ps, pn, C.attc[:, j, 0:1], qh[:, hs], [qn])
                for n in range(8 if STG >= 3 else 0):
                    has_win = not (seg == 0 and n == 0)
                    qs = slice(n * 128, (n + 1) * 128)
                    kwin = slice(n * 128, (n + 1) * 128)
                    kdia = slice((n + 1) * 128, (n + 2) * 128)
                    pe_, pen = pexp[it % 2], L + "pexp%d" % (it % 2)
                    it += 1
                    if not has_win:
                        P.op("pool", lambda e, pe_=pe_: e.memset(pe_[:], 0.0), writes=[pen])
                    for hh in range(2):
                        pr = slice(64 * hh, 64 * hh + 64)
                        psS, psn = next_ps(C)
                        if has_win:
                            P.op("pe", lambda e, pr=pr, psS=psS, kwin=kwin, qs=qs, kv=kv, qh=qh: e.matmul(
                                psS[:, 0:128], khat[pr, kv, kwin], qh[pr, qs], start=True, stop=True),
                                reads=[L + "khat", qn], writes=[psn], sig=False)
                        P.op("pe", lambda e, pr=pr, psS=psS, kdia=kdia, qs=qs, kv=kv, qh=qh: e.matmul(
                            psS[:, 128:256], khat[pr, kv, kdia], qh[pr, qs], start=True, stop=True),
                            reads=[L + "khat", qn], writes=[psn], sig=True)
                        lo = 0 if has_win else 128
                        P.op("act", lambda e, pe_=pe_, psS=psS, hh=hh, lo=lo: e.activation(out=pe_[:, hh * 256 + lo:hh * 256 + 256],
                                                                                         in_=psS[:, lo:256], func=AF.Exp, scale=0.125),
                             reads=[psn], writes=[pen])
                    P.op("dve", lambda e, pe_=pe_, wt=wt: e.tensor_tensor(out=pe_[:], in0=pe_[:], in1=wt[:], op=ALU.mult),
                         reads=[pen, wtn], writes=[pen])
                    if STG < 4:
                        continue
                    psO, pon = next_ps(C)
                    for hh in range(2):
                        po = slice(64 * hh, 64 * hh + 64)
                        vc = slice(kv * 64, kv * 64 + 64)
                        for (is_v, c0) in ((True, 0), (False, 128)):
                            lw = vtok[:, n, vc] if is_v else C.ones[:, 0:64]
                            ld = vtok[:, n + 1, vc] if is_v else C.ones[:, 0:64]
                            if has_win:
                                P.op("pe", lambda e, psO=psO, po=po, c0=c0, lw=lw, pe_=pe_, hh=hh: e.matmul(
                                    psO[po, c0:c0 + 128], lw, pe_[:, hh * 256:hh * 256 + 128], start=True, stop=False),
                                    reads=[L + "vtok", "ones", pen], writes=[pon], sig=False)
                            P.op("pe", lambda e, psO=psO, po=po, c0=c0, ld=ld, pe_=pe_, hh=hh, has_win=has_win: e.matmul(
                                psO[po, c0:c0 + 128], ld, pe_[:, hh * 256 + 128:hh * 256 + 256], start=(not has_win), stop=True),
                                reads=[L + "vtok", "ones", pen], writes=[pon], sig=(hh == 1 and c0 == 128))
                    P.op("dve", lambda e, psO=psO, i=i: e.tensor_scalar(out=rden[:], in0=psO[:, 128:256], scalar1=esink[:, i:i + 1], scalar2=None, op0=ALU.add),
                         reads=[pon, L + "esink"], writes=[L + "rden"])
                    P.op("dve", lambda e: e.reciprocal(out=rden[:], in_=rden[:]), reads=[L + "rden"], writes=[L + "rden"])
                    P.op("dve", lambda e, psO=psO, i=i, qs=qs: e.tensor_tensor(out=C.yT[:, i, qs], in0=psO[:, 0:128], in1=rden[:], op=ALU.mult),
                         reads=[pon, L + "rden"], writes=["yT%d.%d" % (i, n // 4)])
        C.swa_it = it
        P.op("pool", lambda e: e.tensor_copy(out=C.khalo[j][:], in_=khat[:, :, TOK:TOK + 128]), reads=[L + "khat"], writes=["khalo%d" % j])
        P.op("pool", lambda e: e.tensor_copy(out=C.vhalo[j][:], in_=vtok[:, 8, :]), reads=[L + "vtok"], writes=["vhalo%d" % j])
        P.fence()
    proj_add(C, wout, C.yT, "yT")


def host_consts():
    c = {}
    c["c_ones"] = np.ones((128, 128), np.float32)
    c["c_ident"] = np.eye(128, dtype=np.float32)
    bo = np.zeros((128, 128), np.float32)
    bo[:64, :64] = 1.0
    bo[64:, 64:] = 1.0
    c["c_bones"] = bo
    s = np.arange(128)[:, None].astype(np.float64)
    t = np.arange(128)[None, :].astype(np.float64)
    wt = np.zeros((128, 32, 2, 128), np.float64)
    for h in range(32):
        slope = 2.0 ** (-8.0 * (h + 1) / 32)
        wt[:, h, 0, :] = np.where(t < s, np.exp(-slope * (t + 128 - s)), 0.0)
        wt[:, h, 1, :] = np.where(t >= s, np.exp(-slope * (t - s)), 0.0)
    c["c_wt"] = wt.reshape(128, 32 * 2 * 128).astype(np.float32)
    return c


def setup_even(C, dr):
    sb = C.sb
    P = C.P
    C.s5p = sb("s5p", [128, 2, 3, 32], F32)
    C.s5c = sb("s5c", [128, 2, 6, 32], F32)
    C.s5m = sb("s5m", [128, 2, 3, 10, 32], F32)
    C.s5x = sb("s5x", [128, 2, 32, 2], F32)
    C.evc = sb("evc", [128, 2, 27], F32)
    C.lbv = sb("lbv", [128, 2, 2, 8], F32)
    C.hS = sb("hS", [128, 2, 8, 128], F32)
    C.cmask = sb("cmask", [128, TOK], BF16)
    C.tri = sb("tri", [128, 64], F32)
    P.dma("sp", "c_s5p", C.s5p[:], dr["s5p"], writes=["s5p"])
    P.dma("sp", "c_evc", C.evc[:], dr["evc"], writes=["evc"])
    P.dma("pool", "c_cmask", C.cmask[:], dr["c_cmask"], writes=["cmask"])
    P.dma("sp", "c_tri", C.tri[:], dr["c_tri"], writes=["tri"])
    P.op("pool", lambda e: e.memset(C.s5x[:], 0.0), writes=["s5x0", "s5x1"])
    P.op("pool", lambda e: e.memset(C.hS[:], 0.0), writes=["hS0", "hS1"])
    P.op("dve", lambda e: e.memset(C.lbv[:, 0, 0, :], 0.0), writes=["lbv"])
    P.op("dve", lambda e: e.memset(C.lbv[:, 0, 1, :], 1.0), writes=["lbv"])
    P.op("dve", lambda e: e.tensor_tensor(out=C.lbv[:, 1, 1, :], in0=C.evc[:, 1, 16:24], in1=C.evc[:, 0, 16:24], op=ALU.subtract),
         reads=["evc", "lbv"], writes=["lbv"])
    P.op("act", lambda e: e.activation(out=C.lbv[:, 1, 0, :], in_=C.lbv[:, 1, 1, :], func=AF.Sigmoid), reads=["lbv"], writes=["lbv"])
    P.op("dve", lambda e: e.tensor_scalar(out=C.lbv[:, 1, 1, :], in0=C.lbv[:, 1, 0, :], scalar1=-1.0, scalar2=1.0, op0=ALU.mult, op1=ALU.add),
         reads=["lbv"], writes=["lbv"])
    with ExitStack() as st:
        for j in range(2):
            s5_precompute(C, j, st)
        P.fence()


def s5_precompute(C, j, st):
    P = C.P
    n = [0]

    def T():
        n[0] += 1
        return C.sb("s5t%d_%d" % (j, n[0]), [128, 32], F32, st), "s5t%d_%d" % (j, n[0])

    def tt(out, on, a, an, b, bn, op):
        P.op("dve", lambda e: e.tensor_tensor(out=out, in0=a, in1=b, op=op), reads=[an, bn], writes=[on])

    def ts(out, on, a, an, s1, s2, op0, op1):
        P.op("dve", lambda e: e.tensor_scalar(out=out, in0=a, scalar1=s1, scalar2=s2, op0=op0, op1=op1), reads=[an], writes=[on])

    def act(out, on, a, an, func, **kw):
        P.op("act", lambda e: e.activation(out=out, in_=a, func=func, **kw), reads=[an], writes=[on])

    lamr, lami, ldt = C.s5p[:, j, 0, :], C.s5p[:, j, 1, :], C.s5p[:, j, 2, :]
    dt, dtn = T(); act(dt[:], dtn, ldt, "s5p", AF.Exp)
    lr, lrn = T(); ts(lr[:], lrn, lamr, "s5p", -1e-4, None, ALU.min, ALU.bypass)
    t1, t1n = T(); tt(t1[:], t1n, lr[:], lrn, dt[:], dtn, ALU.mult)
    mag, magn = T(); act(mag[:], magn, t1[:], t1n, AF.Exp)
    r, rn = T(); tt(r[:], rn, lami, "s5p", dt[:], dtn, ALU.mult)
    m, mn = T()
    for _ in range(4):
        ts(m[:], mn, r[:], rn, PI, 2 * PI, ALU.is_gt, ALU.mult)
        tt(r[:], rn, r[:], rn, m[:], mn, ALU.subtract)
    sn_, snn = T(); act(sn_[:], snn, r[:], rn, AF.Sin)
    r2, r2n = T(); ts(r2[:], r2n, r[:], rn, PI / 2, None, ALU.add, ALU.bypass)
    ts(m[:], mn, r2[:], r2n, PI, 2 * PI, ALU.is_gt, ALU.mult)
    tt(r2[:], r2n, r2[:], r2n, m[:], mn, ALU.subtract)
    cs, csn = T(); act(cs[:], csn, r2[:], r2n, AF.Sin)
    cn = "s5c%d" % j
    ar, ai, nai, zr, zi, nzi = [C.s5c[:, j, k, :] for k in range(6)]
    tt(ar, cn, mag[:], magn, cs[:], csn, ALU.mult)
    tt(ai, cn, mag[:], magn, sn_[:], snn, ALU.mult)
    ts(nai, cn, ai, cn, -1.0, None, ALU.mult, ALU.bypass)
    d1, d1n = T(); tt(d1[:], d1n, lr[:], lrn, lr[:], lrn, ALU.mult)
    d2, d2n = T(); tt(d2[:], d2n, lami, "s5p", lami, "s5p", ALU.mult)
    tt(d1[:], d1n, d1[:], d1n, d2[:], d2n, ALU.add)
    rd, rdn = T()
    P.op("dve", lambda e: e.reciprocal(out=rd[:], in_=d1[:]), reads=[d1n], writes=[rdn])
    am1, am1n = T(); ts(am1[:], am1n, ar, cn, -1.0, None, ALU.add, ALU.bypass)
    a1, a1n = T(); tt(a1[:], a1n, am1[:], am1n, lr[:], lrn, ALU.mult)
    a2, a2n = T(); tt(a2[:], a2n, ai, cn, lami, "s5p", ALU.mult)
    tt(a1[:], a1n, a1[:], a1n, a2[:], a2n, ALU.add)
    tt(zr, cn, a1[:], a1n, rd[:], rdn, ALU.mult)
    b1, b1n = T(); tt(b1[:], b1n, ai, cn, lr[:], lrn, ALU.mult)
    b2, b2n = T(); tt(b2[:], b2n, am1[:], am1n, lami, "s5p", ALU.mult)
    tt(b1[:], b1n, b1[:], b1n, b2[:], b2n, ALU.subtract)
    tt(zi, cn, b1[:], b1n, rd[:], rdn, ALU.mult)
    ts(nzi, cn, zi, cn, -1.0, None, ALU.mult, ALU.bypass)
    mnm = "s5m%d" % j
    P.op("act", lambda e: e.activation(out=C.s5m[:, j, 0, 0, :], in_=ar, func=AF.Copy), reads=[cn], writes=[mnm])
    P.op("act", lambda e: e.activation(out=C.s5m[:, j, 1, 0, :], in_=ai, func=AF.Copy), reads=[cn], writes=[mnm])
    for k in range(1, 10):
        pr_, pi_ = C.s5m[:, j, 0, k - 1, :], C.s5m[:, j, 1, k - 1, :]
        tt(d1[:], d1n, pr_, mnm, pr_, mnm, ALU.mult)
        tt(d2[:], d2n, pi_, mnm, pi_, mnm, ALU.mult)
        tt(C.s5m[:, j, 0, k, :], mnm, d1[:], d1n, d2[:], d2n, ALU.subtract)
        tt(a1[:], a1n, pr_, mnm, pi_, mnm, ALU.mult)
        ts(C.s5m[:, j, 1, k, :], mnm, a1[:], a1n, 2.0, None, ALU.mult, ALU.bypass)
    for k in range(10):
        P.op("dve", lambda e, k=k: e.tensor_scalar(out=C.s5m[:, j, 2, k, :], in0=C.s5m[:, j, 1, k, :], scalar1=-1.0, scalar2=None, op0=ALU.mult),
             reads=[mnm], writes=[mnm])


def s5_part(C, j, seg, w_in2, w_glu, Bl_d, Cl_d):
    P = C.P
    L = "S%d_%d_" % (seg, j)
    cn, mnm, xn_ = "s5c%d" % j, "s5m%d" % j, "s5x%d" % j
    with ExitStack() as st:
        sb = lambda n, s, d: C.sb(n, s, d, st)
        uT = C.yT[:, 0:8, :]
        Bl = sb("Bl", [128, 8, 2, 128], BF16)
        Cls = [sb("Cl%d" % i, [128, 4, 2, 32], F32) for i in range(2)]
        X = [[sb("x%d%d" % (a, b), [128, TOK], F32) for b in range(2)] for a in range(2)]
        tz = sb("tz", [128, HF], F32)
        P.dma("pool", "Bl", Bl[:], Bl_d, writes=[L + "Bl"])
        for ub in range(4):
            w, wn = load_w(C, w_in2[:, ub * WC:(ub + 1) * WC], KT, WC)
            for m in range(2):
                for hf in range(2):
                    hs = slice(hf * HF, (hf + 1) * HF)
                    ps, pn = mm_fm(C, w, wn, m * 128, C.xn, "xn", hf)
                    P.op("act", lambda e, ps=ps, ub=ub, m=m, hs=hs: e.activation(out=uT[:, ub * 2 + m, hs], in_=ps[:], func=AF.Copy),
                         reads=[pn], writes=[L + "uT%d" % (ub * 2 + m), "yT%d.%d" % (ub * 2 + m, hf)])
        psY = [C.psx[0], C.psx[1]]
        for t in range(8):
            Cl, cln = Cls[t % 2], L + "Cl%d" % (t % 2)
            P.dma("sp", "Cl%d" % (t % 2), Cl[:], Cl_d[:, 4 * t:4 * t + 4], writes=[cln])
            P.op("act", lambda e, Cl=Cl: e.activation(out=Cl[:, :, 1, :], in_=Cl[:, :, 1, :], func=AF.Copy, scale=-1.0), reads=[cln], writes=[cln])
            for q in range(4):
                gp = 4 * t + q
                rows = slice(32 * q, 32 * q + 32)
                zr, zi, nzi = C.s5c[:, j, 3, gp:gp + 1], C.s5c[:, j, 4, gp:gp + 1], C.s5c[:, j, 5, gp:gp + 1]
                ar, ai, nai = C.s5c[:, j, 0, gp:gp + 1], C.s5c[:, j, 1, gp:gp + 1], C.s5c[:, j, 2, gp:gp + 1]
                A, B = X[0], X[1]
                an, bn = [L + "x0r", L + "x0i"], [L + "x1r", L + "x1i"]
                for hf in range(2):
                    hs = slice(hf * HF, (hf + 1) * HF)
                    pr, prn = next_ps(C)
                    pi, pin = next_ps(C)
                    for (pp, ppn, ri) in ((pr, prn, 0), (pi, pin, 1)):
                        P.op("pe", lambda e, pp=pp, ri=ri, rows=rows, t=t, hs=hs, q=q: e.matmul(pp[:], Bl[rows, t, ri, :], uT[rows, t, hs], start=True, stop=True,
                                                                                           tile_position=(32 * q, 0)),
                             reads=[L + "Bl", L + "uT%d" % t], writes=[ppn])
                    P.op("dve", lambda e, pi=pi, zi=zi: e.tensor_scalar(out=tz[:], in0=pi[:], scalar1=zi, scalar2=None, op0=ALU.mult), reads=[pin, cn], writes=[L + "tz"])
                    P.op("dve", lambda e, pr=pr, zr=zr, hs=hs, A=A: e.scalar_tensor_tensor(out=A[0][:, hs], in0=pr[:], scalar=zr, in1=tz[:], op0=ALU.mult, op1=ALU.subtract),
                         reads=[prn, cn, L + "tz"], writes=[an[0]])
                    P.op("dve", lambda e, pr=pr, zi=zi: e.tensor_scalar(out=tz[:], in0=pr[:], scalar1=zi, scalar2=None, op0=ALU.mult), reads=[prn, cn], writes=[L + "tz"])
                    P.op("dve", lambda e, pi=pi, zr=zr, hs=hs, A=A: e.scalar_tensor_tensor(out=A[1][:, hs], in0=pi[:], scalar=zr, in1=tz[:], op0=ALU.mult, op1=ALU.add),
                         reads=[pin, cn, L + "tz"], writes=[an[1]])
                xr, xi = C.s5x[:, j, gp, 0:1], C.s5x[:, j, gp, 1:2]
                P.op("dve", lambda e, A=A, xr=xr, ar=ar: e.scalar_tensor_tensor(out=A[0][:, 0:1], in0=xr, scalar=ar, in1=A[0][:, 0:1], op0=ALU.mult, op1=ALU.add),
                     reads=[xn_, cn, an[0]], writes=[an[0]])
                P.op("dve", lambda e, A=A, xi=xi, nai=nai: e.scalar_tensor_tensor(out=A[0][:, 0:1], in0=xi, scalar=nai, in1=A[0][:, 0:1], op0=ALU.mult, op1=ALU.add),
                     reads=[xn_, cn, an[0]], writes=[an[0]])
                P.op("dve", lambda e, A=A, xi=xi, ar=ar: e.scalar_tensor_tensor(out=A[1][:, 0:1], in0=xi, scalar=ar, in1=A[1][:, 0:1], op0=ALU.mult, op1=ALU.add),
                     reads=[xn_, cn, an[1]], writes=[an[1]])
                P.op("dve", lambda e, A=A, xr=xr, ai=ai: e.scalar_tensor_tensor(out=A[1][:, 0:1], in0=xr, scalar=ai, in1=A[1][:, 0:1], op0=ALU.mult, op1=ALU.add),
                     reads=[xn_, cn, an[1]], writes=[an[1]])
                for k in range(10):
                    d = 1 << k
                    mr, mi, nmi = C.s5m[:, j, 0, k, gp:gp + 1], C.s5m[:, j, 1, k, gp:gp + 1], C.s5m[:, j, 2, k, gp:gp + 1]
                    P.op("act", lambda e, A=A, B=B, d=d: e.activation(out=B[0][:, 0:d], in_=A[0][:, 0:d], func=AF.Copy), reads=[an[0]], writes=[bn[0]])
                    P.op("act", lambda e, A=A, B=B, d=d: e.activation(out=B[1][:, 0:d], in_=A[1][:, 0:d], func=AF.Copy), reads=[an[1]], writes=[bn[1]])
                    P.op("dve", lambda e, A=A, B=B, d=d, mr=mr: e.scalar_tensor_tensor(out=B[0][:, d:], in0=A[0][:, 0:TOK - d], scalar=mr, in1=A[0][:, d:], op0=ALU.mult, op1=ALU.add),
                         reads=[an[0], mnm], writes=[bn[0]])
                    P.op("dve", lambda e, A=A, B=B, d=d, nmi=nmi: e.scalar_tensor_tensor(out=B[0][:, d:], in0=A[1][:, 0:TOK - d], scalar=nmi, in1=B[0][:, d:], op0=ALU.mult, op1=ALU.add),
                         reads=[an[1], mnm, bn[0]], writes=[bn[0]])
                    P.op("dve", lambda e, A=A, B=B, d=d, mr=mr: e.scalar_tensor_tensor(out=B[1][:, d:], in0=A[1][:, 0:TOK - d], scalar=mr, in1=A[1][:, d:], op0=ALU.mult, op1=ALU.add),
                         reads=[an[1], mnm], writes=[bn[1]])
                    P.op("dve", lambda e, A=A, B=B, d=d, mi=mi: e.scalar_tensor_tensor(out=B[1][:, d:], in0=A[0][:, 0:TOK - d], scalar=mi, in1=B[1][:, d:], op0=ALU.mult, op1=ALU.add),
                         reads=[an[0], mnm, bn[1]], writes=[bn[1]])
                    A, B = B, A
                    an, bn = bn, an
                P.op("act", lambda e, A=A, gp=gp: e.activation(out=C.s5x[:, j, gp, 0:1], in_=A[0][:, TOK - 1:TOK], func=AF.Copy), reads=[an[0]], writes=[xn_])
                P.op("act", lambda e, A=A, gp=gp: e.activation(out=C.s5x[:, j, gp, 1:2], in_=A[1][:, TOK - 1:TOK], func=AF.Copy), reads=[an[1]], writes=[xn_])
                for hf in range(2):
                    hs = slice(hf * HF, (hf + 1) * HF)
                    P.op("pe", lambda e, A=A, gp=gp, hs=hs, hf=hf, rows=rows, q=q, Cl=Cl: e.matmul(psY[hf][rows, :], Cl[:, q, 0, :], A[0][:, hs], start=True, stop=False,
                                                                                      tile_position=(0, 32 * q)),
                         reads=[cln, an[0]], writes=["psx%d" % hf], sig=False)
                    P.op("pe", lambda e, A=A, gp=gp, hs=hs, hf=hf, rows=rows, q=q, Cl=Cl: e.matmul(psY[hf][rows, :], Cl[:, q, 1, :], A[1][:, hs], start=False, stop=True,
                                                                                      tile_position=(0, 32 * q)),
                         reads=[cln, an[1]], writes=["psx%d" % hf], sig=True)
            for hf in range(2):
                hs = slice(hf * HF, (hf + 1) * HF)
                P.op("dve", lambda e, t=t, hs=hs, hf=hf: e.scalar_tensor_tensor(out=C.tmpf[:], in0=uT[:, t, hs], scalar=C.evc[:, j, t:t + 1], in1=psY[hf][:], op0=ALU.mult, op1=ALU.add),
                     reads=[L + "uT%d" % t, "evc", "psx%d" % hf], writes=["tmpf"])
                P.op("act", lambda e, t=t, hs=hs: e.activation(out=C.yT[:, 8 + t, hs], in_=C.tmpf[:], func=AF.Gelu), reads=["tmpf"], writes=["yT%d.%d" % (8 + t, hf)])
        P.fence()
    it = 0
    for gb in range(4):
        w, wn = load_w(C, w_glu[:, gb * WC:(gb + 1) * WC], 8, WC)
        for m in range(2):
            mt = gb * 2 + m
            for hf in range(2):
                hs = slice(hf * HF, (hf + 1) * HF)
                ps, pn = next_ps(C)
                for kt in range(8):
                    P.op("pe", lambda e, w=w, m=m, kt=kt, ps=ps, hs=hs: e.matmul(ps[:], w[:, kt, m * 128:(m + 1) * 128], C.yT[:, 8 + kt, hs], start=(kt == 0), stop=(kt == 7)),
                         reads=[wn, "yT%d.%d" % (8 + kt, hf)], writes=[pn], sig=(kt == 7))
                tb, tn = C.tmpb[it % 2], "tmpb%d" % (it % 2)
                it += 1
                P.op("act", lambda e, ps=ps, tb=tb, mt=mt: e.activation(out=tb[:], in_=ps[:], func=AF.Sigmoid, bias=C.evc[:, j, 8 + mt:9 + mt]),
                     reads=[pn, "evc"], writes=[tn])
                P.op("pool", lambda e, tb=tb, mt=mt, hs=hs: e.tensor_tensor(out=C.yT[:, mt, hs], in0=C.yT[:, 8 + mt, hs], in1=tb[:], op=ALU.mult),
                     reads=[tn, "yT%d.%d" % (8 + mt, hf)], writes=["yT%d.%d" % (mt, hf)])


def hgrn_part(C, j, seg, w_in2):
    P = C.P
    sn_ = "hS%d" % j
    with ExitStack() as st:
        sb = lambda n, s, d: C.sb(n, s, d, st)
        qs = sb("qs", [128, TOK], BF16)
        gs = sb("gs", [128, TOK], BF16)
        fg = sb("fg", [128, TOK], F32)
        kk = sb("kk", [128, TOK], BF16)
        bb = sb("bb", [128, TOK], F32)
        t1 = fg
        t2 = sb("t2", [128, TOK], BF16)
        Qm = sb("Qm", [128, TOK], BF16)
        Km = sb("Km", [128, TOK], BF16)
        Qt = sb("Qt", [128, TOK], BF16)
        Kh = sb("Kh", [128, TOK], BF16)
        dec = sb("dec", [128, 16], F32)
        vt = sb("vt", [128, 8, 128], BF16)
        kht = sb("kht", [128, 8, 128], BF16)
        at = [sb("at%d" % i, [128, 64], BF16) for i in range(2)]
        Sb = sb("Sb", [128, 128], BF16)
        osq = sb("osq", [128, HF], BF16)
        for i_ in range(2):
            P.op("pool", lambda e, i_=i_: e.memset(at[i_][:], 0.0), writes=["H%d_%d_at%d" % (seg, j, i_)])
        for h in range(8):
            L = "H%d_%d_" % (seg, j)
            c0 = 1024 + h * 512
            w, wn = load_w(C, w_in2[:, c0:c0 + WC], KT, WC)
            for hf in range(2):
                hs = slice(hf * HF, (hf + 1) * HF)
                ps, pn = mm_fm(C, w, wn, 0, C.xn, "xn", hf)
                P.op("act", lambda e, ps=ps, hs=hs: e.activation(out=qs[:, hs], in_=ps[:], func=AF.Silu), reads=[pn], writes=[L + "qs"])
                ps, pn = mm_fm(C, w, wn, 128, C.xn, "xn", hf)
                P.op("act", lambda e, ps=ps, hs=hs: e.activation(out=fg[:, hs], in_=ps[:], func=AF.Sigmoid), reads=[pn], writes=[L + "fg"])
            w, wn = load_w(C, w_in2[:, c0 + WC:c0 + 2 * WC], KT, WC)
            for hf in range(2):
                hs = slice(hf * HF, (hf + 1) * HF)
                ps, pn = mm_fm(C, w, wn, 128, C.xn, "xn", hf)
                P.op("act", lambda e, ps=ps, hs=hs: e.activation(out=gs[:, hs], in_=ps[:], func=AF.Silu), reads=[pn], writes=[L + "gs"])
            for tb in range(8):
                ps, pn = next_ps(C)
                for kt in range(KT):
                    P.op("pe", lambda e, w=w, tb=tb, kt=kt, ps=ps: e.matmul(ps[:, 0:128], C.xn[:, kt, tb * 128:(tb + 1) * 128], w[:, kt, 0:128],
                                                                          start=(kt == 0), stop=(kt == KT - 1)),
                         reads=[wn, "xn%d.%d" % (kt, tb // 4)], writes=[pn], sig=(kt == KT - 1))
                P.op("act", lambda e, tb=tb, ps=ps: e.activation(out=vt[:, tb, :], in_=ps[:, 0:128], func=AF.Copy), reads=[pn], writes=[L + "vt"])
            P.op("dve", lambda e, h=h: e.tensor_scalar(out=fg[:], in0=fg[:], scalar1=C.lbv[:, j, 1, h:h + 1], scalar2=C.lbv[:, j, 0, h:h + 1], op0=ALU.mult, op1=ALU.add),
                 reads=[L + "fg", "lbv"], writes=[L + "fg"])
            P.op("dve", lambda e: e.tensor_scalar(out=kk[:], in0=fg[:], scalar1=-1.0, scalar2=1.0, op0=ALU.mult, op1=ALU.add), reads=[L + "fg"], writes=[L + "kk"])
            P.op("act", lambda e: e.activation(out=fg[:], in_=fg[:], func=AF.Ln), reads=[L + "fg"], writes=[L + "fg"])
            P.op("dve", lambda e: e.tensor_tensor_scan(out=bb[:], data0=C.cmask[:], data1=fg[:], initial=0.0, op0=ALU.mult, op1=ALU.add),
                 reads=[L + "fg", "cmask"], writes=[L + "bb"])
            b3 = bb[:].rearrange("p (c t) -> p c t", t=64)
            bmid = b3[:, :, 31:32].broadcast_to([128, 16, 64])
            blast = b3[:, :, 63:64].broadcast_to([128, 16, 64])
            t13 = t1[:].rearrange("p (c t) -> p c t", t=64)
            t23 = t2[:].rearrange("p (c t) -> p c t", t=64)
            P.op("dve", lambda e, b3=b3, bmid=bmid, t13=t13: e.tensor_tensor(out=t13, in0=b3, in1=bmid, op=ALU.subtract), reads=[L + "bb"], writes=[L + "fg"])
            P.op("act", lambda e: e.activation(out=t2[:], in_=t1[:], func=AF.Exp), reads=[L + "fg"], writes=[L + "t2"])
            P.op("dve", lambda e: e.tensor_tensor(out=Qm[:], in0=qs[:], in1=t2[:], op=ALU.mult), reads=[L + "qs", L + "t2"], writes=[L + "Qm"])
            P.op("act", lambda e: e.activation(out=t2[:], in_=t1[:], func=AF.Exp, scale=-1.0), reads=[L + "fg", L + "Qm"], writes=[L + "t2"])
            P.op("dve", lambda e: e.tensor_tensor(out=Km[:], in0=kk[:], in1=t2[:], op=ALU.mult), reads=[L + "kk", L + "t2"], writes=[L + "Km"])
            P.op("act", lambda e: e.activation(out=t2[:], in_=bb[:], func=AF.Exp), reads=[L + "bb", L + "Km"], writes=[L + "t2"])
            P.op("dve", lambda e: e.tensor_tensor(out=Qt[:], in0=qs[:], in1=t2[:], op=ALU.mult), reads=[L + "qs", L + "t2"], writes=[L + "Qt"])
            P.op("dve", lambda e, b3=b3, blast=blast, t13=t13: e.tensor_tensor(out=t13, in0=b3, in1=blast, op=ALU.subtract), reads=[L + "bb", L + "fg"], writes=[L + "fg"])
            P.op("act", lambda e: e.activation(out=t2[:], in_=t1[:], func=AF.Exp, scale=-1.0), reads=[L + "fg", L + "Qt"], writes=[L + "t2"])
            P.op("dve", lambda e: e.tensor_tensor(out=Kh[:], in0=kk[:], in1=t2[:], op=ALU.mult), reads=[L + "kk", L + "t2"], writes=[L + "Kh"])
            P.op("act", lambda e, b3=b3: e.activation(out=dec[:], in_=b3[:, :, 63], func=AF.Exp), reads=[L + "bb"], writes=[L + "dec"])
            for tb in range(8):
                P.op("pe", lambda e, tb=tb: e.transpose(C.psT[:, tb * 128:(tb + 1) * 128], Kh[:, tb * 128:(tb + 1) * 128], C.ident[:]),
                     reads=[L + "Kh", "ident"], writes=["psT"], sig=(tb == 7))
            P.op("act", lambda e: e.activation(out=kht[:].rearrange("p a b -> p (a b)"), in_=C.psT[:], func=AF.Copy), reads=["psT"], writes=[L + "kht"])
            P.op("act", lambda e, h=h: e.activation(out=Sb[:], in_=C.hS[:, j, h, :], func=AF.Copy), reads=[sn_], writes=[L + "Sb"])
            psO = None
            for c in range(16):
                tb, cc = c // 2, c % 2
                pr = slice(64 * cc, 64 * cc + 64)
                cs = slice(c * 64, (c + 1) * 64)
                if c % 8 == 0:
                    psO, pon = C.psx[c // 8], "psx%d" % (c // 8)
                psA, pan = next_ps(C)
                P.op("pe", lambda e, psA=psA, pr=pr, cs=cs: e.matmul(psA[pr, 0:64], Km[:, cs], Qm[:, cs], start=True, stop=True),
                     reads=[L + "Km", L + "Qm"], writes=[pan])
                a_, atn = at[c % 2], "H%d_%d_at%d" % (seg, j, c % 2)
                P.op("dve", lambda e, psA=psA, pr=pr, a_=a_: e.tensor_tensor(out=a_[pr, :], in0=psA[pr, 0:64], in1=C.tri[pr, :], op=ALU.mult),
                     reads=[pan, "tri"], writes=[atn])
                oc = slice((c % 8) * 64, (c % 8) * 64 + 64)
                P.op("pe", lambda e, psO=psO, oc=oc, pr=pr, tb=tb, a_=a_: e.matmul(psO[:, oc], vt[:, tb, :], a_[:, :], start=True, stop=False),
                     reads=[L + "vt", atn], writes=[pon], sig=False)
                P.op("pe", lambda e, psO=psO, oc=oc, cs=cs: e.matmul(psO[:, oc], Sb[:], Qt[:, cs], start=False, stop=True),
                     reads=[L + "Sb", L + "Qt"], writes=[pon], sig=True)
                psS, psn = next_ps(C)
                P.op("pe", lambda e, psS=psS, pr=pr, tb=tb: e.matmul(psS[:, 0:128], kht[pr, tb, :], vt[pr, tb, :], start=True, stop=True),
                     reads=[L + "kht", L + "vt"], writes=[psn])
                P.op("dve", lambda e, psS=psS, c=c, h=h: e.scalar_tensor_tensor(out=C.hS[:, j, h, :], in0=C.hS[:, j, h, :], scalar=dec[:, c:c + 1], in1=psS[:, 0:128],
                                                                               op0=ALU.mult, op1=ALU.add),
                     reads=[psn, sn_, L + "dec"], writes=[sn_])
                P.op("act", lambda e, h=h: e.activation(out=Sb[:], in_=C.hS[:, j, h, :], func=AF.Copy), reads=[sn_], writes=[L + "Sb"])
                if c % 8 == 7:
                    hf = c // 8
                    hs = slice(hf * HF, (hf + 1) * HF)
                    P.op("act", lambda e, psO=psO: e.activation(out=osq[:], in_=psO[:], func=AF.Square), reads=[pon], writes=[L + "osq"])
                    p2, p2n = next_ps(C)
                    P.op("pe", lambda e, p2=p2: e.matmul(p2[:], C.ones[:], osq[:], start=True, stop=True), reads=[L + "osq", "ones"], writes=[p2n])
                    P.op("act", lambda e, p2=p2: e.activation(out=C.tmpf[:], in_=p2[:], func=AF.Sqrt, bias=C.epsc[:, 0:1], scale=1.0 / 128),
                         reads=[p2n, "epsc"], writes=["tmpf"])
                    P.op("dve", lambda e: e.reciprocal(out=C.rstd[:], in_=C.tmpf[:]), reads=["tmpf"], writes=["rstd"])
                    P.op("dve", lambda e, psO=psO: e.scalar_tensor_tensor(out=C.tmpf[:], in0=psO[:], scalar=C.evc[:, j, 24:25], in1=C.rstd[:], op0=ALU.mult, op1=ALU.mult),
                         reads=[pon, "rstd", "evc", "tmpf"], writes=["tmpf"])
                    P.op("dve", lambda e, h=h, hs=hs: e.tensor_tensor(out=C.yT[:, 8 + h, hs], in0=C.tmpf[:], in1=gs[:, hs], op=ALU.mult),
                         reads=["tmpf", L + "gs"], writes=["yT%d.%d" % (8 + h, hf)])
        P.fence()


def even_layer(C, j, seg, dr):
    import os
    dbg = os.environ.get("EVEN_DBG", "")
    if dbg == "hgrn":
        for kt in range(8):
            C.P.op("pool", lambda e, kt=kt: e.memset(C.yT[:, kt, :], 0.0), writes=["yT%d.0" % kt, "yT%d.1" % kt])
    else:
        s5_part(C, j, seg, dr["w_in2"][j], dr["s5_w_glu"][j], dr["s5_Bl"][j], dr["s5_Cl"][j])
    if dbg == "s5":
        for kt in range(8, 16):
            C.P.op("pool", lambda e, kt=kt: e.memset(C.yT[:, kt, :], 0.0), writes=["yT%d.0" % kt, "yT%d.1" % kt])
    else:
        hgrn_part(C, j, seg, dr["w_in2"][j])
    proj_add(C, dr["even_w_out"][j], C.yT, "yT")


def host_layout(inp):
    f = lambda k: np.asarray(inp[k], np.float32)
    o = {}
    g = np.zeros((128, 8, KT), np.float32)
    rows = [f("even_norm")[0], f("even_norm")[1], f("odd_norm")[0], f("odd_norm")[1]] + [f("mlp_norm")[i] for i in range(4)]
    for i, r in enumerate(rows):
        g[:, i, :] = r.reshape(KT, 128).T
    o["gains"] = g
    attc = np.zeros((128, 2, 18), np.float32)
    for j in range(2):
        attc[:, j, 0] = np.tile(f("q_norm")[j], 2)
        attc[:, j, 1] = np.tile(f("k_norm")[j], 2)
        sk = f("att_sinks")[j]
        for i in range(16):
            attc[:64, j, 2 + i] = sk[2 * i]
            attc[64:, j, 2 + i] = sk[2 * i + 1]
    o["attc"] = attc
    wq = f("odd_w_qkv")
    kd = wq[:, :, 2048:2304].reshape(2, 2048, 4, 1, 64)
    kd = np.broadcast_to(kd, (2, 2048, 4, 2, 64)).reshape(2, 2048, 512)
    o["w_qkv2"] = np.ascontiguousarray(np.concatenate([wq[:, :, :2048], kd, wq[:, :, 2304:2560]], axis=2))
    for k in ("odd_w_out", "mlp_w_up", "mlp_w_down", "even_w_out", "s5_w_glu"):
        o[k] = f(k)
    wi = f("even_w_in")
    hg = wi[:, :, 1024:].reshape(2, 2048, 4, 8, 128).transpose(0, 1, 3, 2, 4).reshape(2, 2048, 4096)
    o["w_in2"] = np.ascontiguousarray(np.concatenate([wi[:, :, :1024], hg], axis=2))
    s5p = np.zeros((128, 2, 3, 32), np.float32)
    for j in range(2):
        for (k, name) in ((0, "s5_lambda_re"), (1, "s5_lambda_im")):
            a = f(name)[j].reshape(32, 2, 64)
            s5p[:, j, k, :] = a.transpose(1, 2, 0).reshape(128, 32)
        ld = f("s5_log_dt")[j].reshape(32, 2)
        s5p[:, j, 2, :] = np.repeat(ld.T[:, None, :], 64, axis=1).reshape(128, 32)
    o["s5p"] = s5p
    Bl = np.zeros((2, 128, 8, 2, 128), np.float32)
    Cl = np.zeros((2, 128, 32, 2, 32), np.float32)
    for j in range(2):
        for ri, (bn_, cn_) in enumerate((("s5_b_re", "s5_c_re"), ("s5_b_im", "s5_c_im"))):
            b = f(bn_)[j]
            c = f(cn_)[j]
            for g_ in range(64):
                gp, g2 = g_ // 2, g_ % 2
                t, q = gp // 4, gp % 4
                Bl[j, q * 32 + g2 * 16:q * 32 + g2 * 16 + 16, t, ri, g2 * 64:(g2 + 1) * 64] = b[g_].T
                Cl[j, g2 * 64:(g2 + 1) * 64, gp, ri, g2 * 16:(g2 + 1) * 16] = c[g_].T
    o["s5_Bl"] = Bl
    o["s5_Cl"] = Cl
    evc = np.zeros((128, 2, 27), np.float32)
    for j in range(2):
        evc[:, j, 0:8] = f("s5_d")[j].reshape(8, 128).T
        evc[:, j, 8:16] = f("s5_b_glu")[j].reshape(8, 128).T
        evc[:, j, 16:24] = f("hgrn_lower_bound")[j].reshape(8, 128).T
        evc[:, j, 24] = f("hgrn_o_norm")[j]
    o["evc"] = evc
    return o


def host_consts2(c):
    cm = np.ones((128, TOK), np.float32)
    cm[:, ::64] = 0.0
    c["c_cmask"] = cm
    s = (np.arange(128) % 64)[:, None]
    t = np.arange(64)[None, :]
    c["c_tri"] = (s <= t).astype(np.float32)
    return c


def build(nseg, layers, with_mlp, shapes):
    T = nseg * TOK
    nc = bass.Bass("TRN2", target_bir_lowering=False)
    dr = {}
    xT = nc.dram_tensor("x_in", [D, T], F32, kind="ExternalInput").ap()
    yT = nc.dram_tensor("y_out", [D, T], F32, kind="ExternalOutput").ap()
    for k, v in shapes.items():
        if k != "x_in":
            dr[k] = nc.dram_tensor(k, list(v), F32, kind="ExternalInput").ap()
    with ExitStack() as es:
        C = setup(nc, es, nseg, T)
        load_consts(C, dr)
        setup_swa(C, dr)
        setup_even(C, dr)
        fin = []
        for seg in range(nseg):
            load_x(C, xT, seg)
            for (kind, j, mi) in layers:
                if kind == "odd":
                    rmsnorm(C, 2 + j)
                    swa_layer(C, j, seg, dr["w_qkv2"][j], dr["odd_w_out"][j], dr["c_wt"])
                else:
                    rmsnorm(C, j)
                    even_layer(C, j, seg, dr)
                if with_mlp:
                    rmsnorm(C, 4 + mi)
                    mlp(C, dr["mlp_w_up"][mi], dr["mlp_w_down"][mi])
            fin += store_y(C, yT, seg)
        C.P.emit(final_waits=fin[-1:])
        print("instr counts", {e: len(q) for e, q in C.P.q.items()})
    return nc


_CACHE = {}


def kernel(**inputs):
    x = np.asarray(inputs["x"], np.float32)
    hl = host_layout(inputs)
    hc = host_consts2(host_consts())
    base = dict(hl)
    base.update(hc)
    nseg = x.shape[1] // TOK
    shapes = {k: v.shape for k, v in base.items()}
    layers = [("even", 0, 0), ("odd", 0, 1), ("even", 1, 2), ("odd", 1, 3)]
    if "nc" not in _CACHE:
        _CACHE["nc"] = build(nseg, layers, True, shapes)
    nc = _CACHE["nc"]
    in_maps = []
    for c in range(8):
        m = dict(base)
        m["x_in"] = np.ascontiguousarray(x[c % x.shape[0]].T)
        in_maps.append(m)
    res = run_bass_kernel_spmd(nc, in_maps, core_ids=list(range(8)))
    out = np.stack([np.ascontiguousarray(res.results[b]["y_out"].T) for b in range(x.shape[0])], axis=0)
    return out.astype(np.float32)
```
